# Optimizing a Trainium2 kernel written in Bass

```python
import math
import jax
import jax.numpy as jnp
from jax import lax
import numpy as np

D_MODEL = 2048
BATCH = 2
SEQ = 4096
DEPTH = 2

GRID_W = 64
CTX_LEN = 256
HEAD_DIM = 128
NA_W = D_MODEL // 2
NA_HEADS = NA_W // HEAD_DIM
WIN_H_MAX = 8
WIN_W = 16
LRU_W = D_MODEL // 4
LRU_BLOCKS = 4
LRU_BW = LRU_W // LRU_BLOCKS
CONV_W = 4
LRU_C = 8.0
FNET_W = D_MODEL // 4
FNET_GROUPS = 4
FNET_GW = FNET_W // FNET_GROUPS
MIX_W = NA_W + LRU_W + FNET_W
Q0 = 0
K0 = NA_W
V0 = 2 * NA_W
X0 = 3 * NA_W
G0 = X0 + LRU_W
F0 = G0 + LRU_W
IN_W = F0 + FNET_W
D_FF = 4 * D_MODEL
ROPE_THETA = 10000.0
LN_EPS = 1e-5
NEG_INF = -1e30

kernel_name = "hybrid_na_rglru_fnet_deepnorm_dit"


def layer_norm(x, g=None, b=None):
    xf = x.astype(jnp.float32)
    mu = jnp.mean(xf, -1, keepdims=True)
    var = jnp.mean(jnp.square(xf - mu), -1, keepdims=True)
    y = (xf - mu) * lax.rsqrt(var + LN_EPS)
    if g is not None:
        y = y * g.astype(jnp.float32) + b.astype(jnp.float32)
    return y.astype(x.dtype)


def axial_rope(x, rows, cols):
    half = HEAD_DIM // 2
    quarter = half // 2
    inv = ROPE_THETA ** (-jnp.arange(quarter, dtype=jnp.float32) / quarter)

    def rot(xa, pos):
        ang = pos.astype(jnp.float32)[:, None] * inv
        cos = jnp.cos(ang)[None, :, None, :]
        sin = jnp.sin(ang)[None, :, None, :]
        x1, x2 = xa[..., :quarter], xa[..., quarter:]
        return jnp.concatenate([x1 * cos - x2 * sin, x1 * sin + x2 * cos], -1)

    xf = x.astype(jnp.float32)
    out = jnp.concatenate([rot(xf[..., :half], rows), rot(xf[..., half:], cols)], -1)
    return out.astype(x.dtype)


def neighbourhood_attention(q, k, v, kc, vc, rpb):
    B, L, H, d = q.shape
    rows = L // GRID_W
    kh = min(WIN_H_MAX, rows)
    scale = d ** -0.5
    t = jnp.arange(L)
    qr = axial_rope(q, t // GRID_W, t % GRID_W)
    kr = axial_rope(k, t // GRID_W, t % GRID_W)
    grid = lambda a: a.reshape(B, rows, GRID_W, H, d)
    qg, qrg, krg, vg = grid(q), grid(qr), grid(kr), grid(v)
    r = jnp.arange(rows)
    row_start = jnp.clip(r - kh // 2, 0, rows - kh)
    row_idx = row_start[:, None] + jnp.arange(kh)
    k_band = krg[:, row_idx]
    v_band = vg[:, row_idx]
    s_band = jnp.einsum('brqhd,brkchd->bhrqkc', qrg, k_band).astype(jnp.float32) * scale
    col = jnp.arange(GRID_W)
    col_start = jnp.clip(col - WIN_W // 2, 0, GRID_W - WIN_W)
    in_win = (col[None, :] >= col_start[:, None]) & (col[None, :] < col_start[:, None] + WIN_W)
    dr = row_idx - r[:, None] + (WIN_H_MAX - 1)
    dc = jnp.clip(col[None, :] - col[:, None] + (WIN_W - 1), 0, 2 * WIN_W - 2)
    bias = rpb[:, dr[:, None, :, None], dc[None, :, None, :]].astype(jnp.float32)
    s_band = jnp.where(in_win[:, None, :], s_band + bias[None], NEG_INF)
    s_ctx = jnp.einsum('brqhd,bchd->bhrqc', qg, kc).astype(jnp.float32) * scale
    n_band = kh * GRID_W
    s = jnp.concatenate([s_band.reshape(B, H, rows, GRID_W, n_band), s_ctx], -1)
    p = jax.nn.softmax(s, axis=-1).astype(v.dtype)
    p_band = p[..., :n_band].reshape(B, H, rows, GRID_W, kh, GRID_W)
    p_ctx = p[..., n_band:]
    out = (jnp.einsum('bhrqkc,brkchd->brqhd', p_band, v_band)
           + jnp.einsum('bhrqc,bchd->brqhd', p_ctx, vc))
    return out.reshape(B, L, H * d)


def context_attention(qc, kc, vc):
    B, C, H, d = qc.shape
    s = jnp.einsum('bqhd,bkhd->bhqk', qc, kc).astype(jnp.float32) * (d ** -0.5)
    p = jax.nn.softmax(s, axis=-1).astype(vc.dtype)
    return jnp.einsum('bhqk,bkhd->bqhd', p, vc).reshape(B, C, H * d)


def centred_conv(x, w, b):
    L = x.shape[1]
    left = CONV_W // 2
    xp = jnp.pad(x, ((0, 0), (left, CONV_W - 1 - left), (0, 0)))
    out = xp[:, 0:L] * w[0]
    for j in range(1, CONV_W):
        out = out + xp[:, j:j + L] * w[j]
    return out + b


def block_diag(x, w, b):
    xb = x.reshape(*x.shape[:-1], LRU_BLOCKS, LRU_BW)
    return jnp.einsum('blnc,ncd->blnd', xb, w).reshape(x.shape) + b


def rglru_coeffs(x, wa, ba, wx, bx, lam):
    xf = x.astype(jnp.float32)
    r = jax.nn.sigmoid(block_diag(xf, wa, ba).astype(jnp.float32))
    i = jax.nn.sigmoid(block_diag(xf, wx, bx).astype(jnp.float32))
    log_a = -LRU_C * r * jax.nn.softplus(-lam.astype(jnp.float32))
    a = jnp.exp(log_a)
    u = jnp.sqrt(-jnp.expm1(2.0 * log_a)) * (i * xf)
    return a, u


def _scan_combine(e1, e2):
    a1, b1 = e1
    a2, b2 = e2
    return a1 * a2, a2 * b1 + b2


def linear_scan(a, u, h0, reverse):
    if reverse:
        a, u = jnp.flip(a, 1), jnp.flip(u, 1)
    u = u.at[:, 0].add(a[:, 0] * h0)
    _, h = lax.associative_scan(_scan_combine, (a, u), axis=1)
    h_last = h[:, -1]
    if reverse:
        h = jnp.flip(h, 1)
    return h, h_last


def bidirectional_rglru(xl, xc, conv_w, conv_b, wa, ba, wx, bx, lam):
    xl = centred_conv(xl, conv_w, conv_b)
    xc = centred_conv(xc, conv_w, conv_b)
    h0 = jnp.zeros((xc.shape[0], LRU_W), jnp.float32)
    y_lat, y_ctx = None, None
    for d, rev in enumerate((False, True)):
        a_c, u_c = rglru_coeffs(xc, wa[d], ba[d], wx[d], bx[d], lam[d])
        h_c, h_last = linear_scan(a_c, u_c, h0, rev)
        a_l, u_l = rglru_coeffs(xl, wa[d], ba[d], wx[d], bx[d], lam[d])
        h_l, _ = linear_scan(a_l, u_l, h_last, rev)
        y_lat = h_l if y_lat is None else y_lat + h_l
        y_ctx = h_c if y_ctx is None else y_ctx + h_c
    return y_lat.astype(xl.dtype), y_ctx.astype(xc.dtype)


def fourier_mix(x, w, b):
    B, L, _ = x.shape
    xg = x.astype(jnp.float32).reshape(B, L, FNET_GROUPS, FNET_GW)
    y = jnp.real(jnp.fft.fft2(xg, axes=(1, 3), norm='ortho')).reshape(B, L, FNET_W)
    return y.astype(x.dtype) @ w + b


def token_mixers(u_lat, u_ctx, w_in, rpb, conv_w, conv_b, wa, ba, wx, bx, lam, fno_w, fno_b, w_out, ctx_out):
    heads = lambda t: t.reshape(*t.shape[:2], NA_HEADS, HEAD_DIM)
    p_lat = u_lat @ w_in
    q, k, v = heads(p_lat[..., Q0:K0]), heads(p_lat[..., K0:V0]), heads(p_lat[..., V0:X0])
    if ctx_out:
        p_ctx = u_ctx @ w_in
    else:
        p_ctx = u_ctx @ w_in[:, K0:G0]
        p_ctx = jnp.pad(p_ctx, ((0, 0), (0, 0), (K0, IN_W - G0)))[..., :G0] if False else None
    if ctx_out:
        kc, vc, xrc = heads(p_ctx[..., K0:V0]), heads(p_ctx[..., V0:X0]), p_ctx[..., X0:G0]
    else:
        pc = u_ctx @ w_in[:, K0:G0]
        kc, vc, xrc = heads(pc[..., :NA_W]), heads(pc[..., NA_W:2 * NA_W]), pc[..., 2 * NA_W:]
    na_lat = neighbourhood_attention(q, k, v, kc, vc, rpb)
    h_lat, h_ctx = bidirectional_rglru(p_lat[..., X0:G0], xrc, conv_w, conv_b, wa, ba, wx, bx, lam)
    lru_lat = h_lat * jax.nn.gelu(p_lat[..., G0:F0])
    f_lat = fourier_mix(p_lat[..., F0:IN_W], fno_w, fno_b)
    m_lat = jnp.concatenate([na_lat, lru_lat, f_lat], -1) @ w_out
    if not ctx_out:
        return m_lat, None
    na_ctx = context_attention(heads(p_ctx[..., Q0:K0]), kc, vc)
    lru_ctx = h_ctx * jax.nn.gelu(p_ctx[..., G0:F0])
    f_ctx = fourier_mix(p_ctx[..., F0:IN_W], fno_w, fno_b)
    m_ctx = jnp.concatenate([na_ctx, lru_ctx, f_ctx], -1) @ w_out
    return m_lat, m_ctx


def sq_relu_mlp(u, w1, b1, w2, b2):
    return jnp.square(jax.nn.relu(u @ w1 + b1)) @ w2 + b2


def setup_inputs(seed: int = 0) -> dict:
    key = jax.random.key(seed)
    ks = jax.random.split(key, 32)
    f32 = jnp.float32
    nrm = lambda k, shape, s: jax.random.normal(k, shape, f32) * s
    beta = (8.0 * DEPTH) ** -0.25
    u = jax.random.uniform(ks[14], (DEPTH, 2, LRU_W), f32, 0.9, 0.999)
    a0 = u ** (1.0 / LRU_C)
    lam = jnp.log(a0) - jnp.log1p(-a0)
    return {
        "x": nrm(ks[0], (BATCH, SEQ, D_MODEL), 1.0),
        "c": nrm(ks[1], (BATCH, D_MODEL), 1.0),
        "ctx": nrm(ks[2], (BATCH, CTX_LEN, D_MODEL), 1.0),
        "c_ctx": nrm(ks[3], (D_MODEL,), 1.0),
        "w_mod": nrm(ks[4], (DEPTH, D_MODEL, 6 * D_MODEL), 0.5 * D_MODEL ** -0.5),
        "b_mod": nrm(ks[5], (DEPTH, 6 * D_MODEL), 0.02),
        "w_in": nrm(ks[6], (DEPTH, D_MODEL, IN_W), D_MODEL ** -0.5),
        "rpb": nrm(ks[7], (DEPTH, NA_HEADS, 2 * WIN_H_MAX - 1, 2 * WIN_W - 1), 0.1),
        "conv_w": nrm(ks[8], (DEPTH, CONV_W, LRU_W), CONV_W ** -0.5),
        "conv_b": nrm(ks[9], (DEPTH, LRU_W), 0.02),
        "lru_wa": nrm(ks[10], (DEPTH, 2, LRU_BLOCKS, LRU_BW, LRU_BW), LRU_BW ** -0.5),
        "lru_ba": nrm(ks[11], (DEPTH, 2, LRU_W), 0.02),
        "lru_wx": nrm(ks[12], (DEPTH, 2, LRU_BLOCKS, LRU_BW, LRU_BW), LRU_BW ** -0.5),
        "lru_bx": nrm(ks[13], (DEPTH, 2, LRU_W), 0.02),
        "lru_lambda": lam,
        "fno_w": nrm(ks[15], (DEPTH, FNET_W, FNET_W), FNET_W ** -0.5),
        "fno_b": nrm(ks[16], (DEPTH, FNET_W), 0.02),
        "w_out": nrm(ks[17], (DEPTH, MIX_W, D_MODEL), beta * MIX_W ** -0.5),
        "ln1_g": 1.0 + nrm(ks[18], (DEPTH, D_MODEL), 0.02),
        "ln1_b": nrm(ks[19], (DEPTH, D_MODEL), 0.02),
        "w_fc1": nrm(ks[20], (DEPTH, D_MODEL, D_FF), D_MODEL ** -0.5),
        "b_fc1": nrm(ks[21], (DEPTH, D_FF), 0.02),
        "w_fc2": nrm(ks[22], (DEPTH, D_FF, D_MODEL), beta * D_FF ** -0.5),
        "b_fc2": nrm(ks[23], (DEPTH, D_MODEL), 0.02),
        "ln2_g": 1.0 + nrm(ks[24], (DEPTH, D_MODEL), 0.02),
        "ln2_b": nrm(ks[25], (DEPTH, D_MODEL), 0.02),
    }


def reference(x, c, ctx, c_ctx, w_mod, b_mod, w_in, rpb, conv_w, conv_b, lru_wa, lru_ba, lru_wx, lru_bx,
              lru_lambda, fno_w, fno_b, w_out, ln1_g, ln1_b, w_fc1, b_fc1, w_fc2, b_fc2, ln2_g, ln2_b):
    alpha = (2.0 * DEPTH) ** 0.25
    for l in range(DEPTH):
        ctx_out = l < DEPTH - 1
        mod_lat = jax.nn.silu(c) @ w_mod[l] + b_mod[l]
        mod_ctx = jax.nn.silu(c_ctx[None]) @ w_mod[l] + b_mod[l]
        sh1, sc1, g1, sh2, sc2, g2 = jnp.split(mod_lat[:, None], 6, axis=-1)
        csh1, csc1, cg1, csh2, csc2, cg2 = jnp.split(mod_ctx[:, None], 6, axis=-1)
        u_lat = layer_norm(x) * (1.0 + sc1) + sh1
        u_ctx = layer_norm(ctx) * (1.0 + csc1) + csh1
        m_lat, m_ctx = token_mixers(u_lat, u_ctx, w_in[l], rpb[l], conv_w[l], conv_b[l], lru_wa[l], lru_ba[l],
                                    lru_wx[l], lru_bx[l], lru_lambda[l], fno_w[l], fno_b[l], w_out[l], ctx_out)
        x = layer_norm(alpha * x + g1 * m_lat, ln1_g[l], ln1_b[l])
        v_lat = layer_norm(x) * (1.0 + sc2) + sh2
        x = layer_norm(alpha * x + g2 * sq_relu_mlp(v_lat, w_fc1[l], b_fc1[l], w_fc2[l], b_fc2[l]), ln2_g[l], ln2_b[l])
        if ctx_out:
            ctx = layer_norm(alpha * ctx + cg1 * m_ctx, ln1_g[l], ln1_b[l])
            v_ctx = layer_norm(ctx) * (1.0 + csc2) + csh2
            ctx = layer_norm(alpha * ctx + cg2 * sq_relu_mlp(v_ctx, w_fc1[l], b_fc1[l], w_fc2[l], b_fc2[l]),
                             ln2_g[l], ln2_b[l])
    return x
```

```python
import math
from contextlib import ExitStack

import numpy as np
import ml_dtypes

import concourse.bass as bass
import concourse.mybir as mybir
from concourse.bass_utils import run_bass_kernel_spmd

F32 = mybir.dt.float32
BF16 = mybir.dt.bfloat16
AF = mybir.ActivationFunctionType
ALU = mybir.AluOpType
AX = mybir.AxisListType

D = 2048
B = 2
SEQ = 4096
DEPTH = 2
GW = 64
CTX = 256
HD = 128
NAW = 1024
NH = 8
LRUW = 512
FNW = 512
INW = 4608
DFF = 8192
EPS = 1e-5
ALPHA = (2.0 * DEPTH) ** 0.25
NCORES = 8
TL = 1024
TC = 64
TT = TL + TC
NEG = -30000.0


class Sched:
    ENG = ("pe", "act", "dve", "pool", "sp")

    def __init__(self, nc, n_dma_sems=40):
        self.nc = nc
        self.ops = {e: [] for e in self.ENG}
        self.res = {}
        self.n_dma_sems = n_dma_sems
        self.dma_rr = 0
        self.dma_val = [0] * n_dma_sems
        self.out_events = []

    def _deps(self, reads, writes):
        deps = []
        for r in reads:
            st = self.res.get(r)
            if st and st["w"] is not None:
                deps.append(st["w"])
        for w in writes:
            st = self.res.get(w)
            if st:
                if st["w"] is not None:
                    deps.append(st["w"])
                deps.extend(st["r"])
        return deps

    def _commit(self, ev, reads, writes):
        for r in reads:
            st = self.res.setdefault(r, {"w": None, "r": []})
            st["r"].append(ev)
        for w in writes:
            self.res[w] = {"w": ev, "r": []}

    def op(self, eng, fn, reads=(), writes=()):
        rec = {"kind": "op", "eng": eng, "fn": fn, "deps": self._deps(reads, writes), "signal": False,
               "idx": None}
        self.ops[eng].append(rec)
        self._commit(rec, reads, writes)
        return rec

    def dma(self, queue, out, in_, reads=(), writes=(), is_output=False):
        s = self.dma_rr
        self.dma_rr = (self.dma_rr + 1) % self.n_dma_sems
        prev = self.dma_val[s]
        self.dma_val[s] += 16
        rec = {"kind": "dma", "eng": queue, "out": out, "in": in_, "deps": self._deps(reads, writes),
               "sem": s, "val": self.dma_val[s], "prev": prev}
        self.ops[queue].append(rec)
        self._commit(rec, reads, writes)
        if is_output:
            self.out_events.append(rec)
        return rec

    def fence(self):
        deps = []
        for e in self.ENG:
            lastop = None
            for rec in self.ops[e]:
                if rec["kind"] == "op":
                    lastop = rec
                elif not rec.get("fenced"):
                    deps.append(rec)
                    rec["fenced"] = True
            if lastop is not None:
                deps.append(lastop)
        for e in self.ENG:
            rec = {"kind": "op", "eng": e, "fn": (lambda eng: eng.nop()), "deps": list(deps), "signal": False,
                   "idx": None}
            self.ops[e].append(rec)

    def emit(self):
        nc = self.nc
        for e in self.ENG:
            for rec in self.ops[e]:
                for d in rec["deps"]:
                    if d["kind"] == "op":
                        d["signal"] = True
        for e in self.ENG:
            c = 0
            for rec in self.ops[e]:
                if rec["kind"] == "op" and rec["signal"]:
                    c += 1
                    rec["idx"] = c
        with ExitStack() as es:
            esem = {e: es.enter_context(nc.semaphore("s_" + e)) for e in self.ENG}
            dsem = [es.enter_context(nc.semaphore("d_%d" % i)) for i in range(self.n_dma_sems)]
            block = es.enter_context(nc.Block())

            def run(e, eng):
                waited = {}
                for rec in self.ops[e]:
                    for d in rec["deps"]:
                        if d["kind"] == "op":
                            key, val, sem = ("e", d["eng"]), d["idx"], esem[d["eng"]]
                            if d["eng"] == e and e == "pe":
                                continue
                        else:
                            key, val, sem = ("d", d["sem"]), d["val"], dsem[d["sem"]]
                        if waited.get(key, 0) >= val:
                            continue
                        eng.wait_ge(sem, val)
                        waited[key] = val
                    if rec["kind"] == "op":
                        ins = rec["fn"](eng)
                        if rec["signal"]:
                            ins.then_inc(esem[e], 1)
                    else:
                        s = rec["sem"]
                        if rec["prev"] > 0 and waited.get(("d", s), 0) < rec["prev"]:
                            eng.wait_ge(dsem[s], rec["prev"])
                            waited[("d", s)] = rec["prev"]
                        eng.dma_start(out=rec["out"], in_=rec["in"]).then_inc(dsem[s], 16)
                if e == "sp":
                    for rec in self.out_events:
                        if waited.get(("d", rec["sem"]), 0) < rec["val"]:
                            eng.wait_ge(dsem[rec["sem"]], rec["val"])
                            waited[("d", rec["sem"])] = rec["val"]

            @block.tensor
            def _(eng):
                run("pe", eng)

            @block.scalar
            def _(eng):
                run("act", eng)

            @block.vector
            def _(eng):
                run("dve", eng)

            @block.gpsimd
            def _(eng):
                run("pool", eng)

            @block.sync
            def _(eng):
                run("sp", eng)


class Ctx:
    N = [0]

    def __init__(self, nc, es):
        self.nc = nc
        self.es = es

    def sb(self, shape, dt, name=None):
        Ctx.N[0] += 1
        return self.es.enter_context(self.nc.sbuf_tensor("%s_%d" % (name or "t", Ctx.N[0]), list(shape), dt))

    def ps(self, shape, dt, name=None):
        Ctx.N[0] += 1
        return self.es.enter_context(self.nc.psum_tensor("%s_%d" % (name or "p", Ctx.N[0]), list(shape), dt))


def layer_norm_chunk(S, cx, x_ap, x_key, out_ap, out_key, rows, tmp, tag):
    stats, mv, rstd = tmp["stats"], tmp["mv"], tmp["rstd"]
    k = ("lnstat", tag)
    for i in range(4):
        S.op("dve", lambda e, i=i: e.bn_stats(out=stats[:rows, i, :], in_=x_ap[:rows, i * 512:(i + 1) * 512]),
             reads=[x_key], writes=[(k, i)])
    S.op("dve", lambda e: e.bn_aggr(out=mv[:rows, :], in_=stats[:rows].rearrange("p a b -> p (a b)")),
         reads=[(k, i) for i in range(4)], writes=[(k, "mv")])
    S.op("act", lambda e: e.activation(out=rstd[:rows, :], in_=mv[:rows, 1:2], func=AF.Sqrt, bias=tmp["eps"][:rows, :],
                                       scale=1.0),
         reads=[(k, "mv"), ("eps", tmp["tag"])], writes=[(k, "sd")])
    S.op("dve", lambda e: e.reciprocal(out=rstd[:rows, :], in_=rstd[:rows, :]), reads=[(k, "sd")], writes=[(k, "sd")])
    S.op("dve", lambda e: e.tensor_scalar(out=out_ap[:rows], in0=x_ap[:rows], scalar1=mv[:rows, 0:1],
                                          scalar2=rstd[:rows, 0:1], op0=ALU.subtract, op1=ALU.mult),
         reads=[x_key, (k, "sd"), (k, "mv")], writes=[out_key])


def ln_tmp(cx, S, tag):
    t = {"stats": cx.sb([128, 4, 6], F32, "st" + tag), "mv": cx.sb([128, 2], F32, "mv" + tag),
         "rstd": cx.sb([128, 1], F32, "rs" + tag), "eps": cx.sb([128, 1], F32, "eps" + tag), "tag": tag}
    S.op("dve", lambda e: e.memset(t["eps"][:], EPS), writes=[("eps", tag)])
    return t


MCOLS = 6 * D // NCORES


def build_M():
    nc = bass.Bass("TRN2", target_bir_lowering=False)
    cT = nc.dram_tensor("cT", [128, 16, 4], F32, kind="ExternalInput").ap()
    wm = nc.dram_tensor("wm", [DEPTH, D, MCOLS], F32, kind="ExternalInput").ap()
    bm = nc.dram_tensor("bm", [DEPTH, 4, MCOLS], F32, kind="ExternalInput").ap()
    out = nc.dram_tensor("out", [DEPTH, 4, MCOLS], F32, kind="ExternalOutput").ap()
    with ExitStack() as es:
        cx = Ctx(nc, es)
        S = Sched(nc)
        ct = cx.sb([128, 16, 4], F32)
        st = cx.sb([128, 16, 4], F32)
        sg = cx.sb([128, 16, 4], F32)
        bt = cx.sb([4, DEPTH, MCOLS], F32)
        ot = cx.sb([4, DEPTH, MCOLS], F32)
        wts = [cx.sb([128, 16, 512], F32, "w") for _ in range(2)]
        pss = [cx.ps([128, 512], F32) for _ in range(2)]
        S.dma("sp", ct[:], cT[:, :, :], writes=["ct"])
        for l in range(DEPTH):
            S.dma("sp", bt[:, l, :], bm[l], writes=[("bt", l)])
        S.op("act", lambda e: e.activation(out=sg[:], in_=ct[:], func=AF.Sigmoid), reads=["ct"], writes=["sg"])
        S.op("dve", lambda e: e.tensor_tensor(out=st[:], in0=ct[:], in1=sg[:], op=ALU.mult), reads=["ct", "sg"],
             writes=["st"])
        i = 0
        for l in range(DEPTH):
            for cb in range(MCOLS // 512):
                w = wts[i % 2]
                ps = pss[i % 2]
                S.dma("sp" if i % 2 == 0 else "pool", w[:],
                      wm[l, :, cb * 512:(cb + 1) * 512].rearrange("(k p) n -> p k n", p=128), writes=[("w", i % 2)])
                for k in range(16):
                    S.op("pe", lambda e, w=w, ps=ps, k=k: e.matmul(ps[0:4, :], st[:, k, :], w[:, k, :], start=(k == 0),
                                                                    stop=(k == 15)),
                         reads=["st", ("w", i % 2)], writes=[("ps", i % 2)])
                S.op("dve", lambda e, ps=ps, l=l, cb=cb: e.tensor_tensor(out=ot[:, l, cb * 512:(cb + 1) * 512],
                                                                        in0=ps[0:4, :],
                                                                        in1=bt[:, l, cb * 512:(cb + 1) * 512],
                                                                        op=ALU.add),
                     reads=[("ps", i % 2), ("bt", l)], writes=[("ot", l, cb)])
                i += 1
        for l in range(DEPTH):
            S.dma("sp", out[l], ot[:, l, :], reads=[("ot", l, cb) for cb in range(3)], is_output=True)
        S.emit()
    return nc


def run_M(c, c_ctx, w_mod, b_mod):
    cvec = np.zeros((4, D), np.float32)
    cvec[0], cvec[1], cvec[2] = c[0], c[1], c_ctx
    cT = np.ascontiguousarray(cvec.T.reshape(16, 128, 4).transpose(1, 0, 2))
    in_maps = []
    for i in range(NCORES):
        sl = slice(i * MCOLS, (i + 1) * MCOLS)
        in_maps.append({"cT": cT, "wm": np.ascontiguousarray(w_mod[:, :, sl]),
                        "bm": np.ascontiguousarray(np.broadcast_to(b_mod[:, None, sl], (DEPTH, 4, MCOLS)))})
    nc = build_M()
    res = run_bass_kernel_spmd(nc, in_maps, core_ids=list(range(NCORES)))
    mod = np.concatenate([r["out"] for r in res.results], axis=2)
    return mod


CH = [(i * 128, 128) for i in range(8)] + [(1024, 64)]
TB = [(0, 512), (512, 512), (1024, 64)]


def ln_transpose_phase(S, cx, x_dram, modfm, uT, ident, lt, tag, mkey="modfm"):
    nc = cx.nc
    xts = [cx.sb([128, D], F32, "xt") for _ in range(2)]
    xhs = [cx.sb([128, D], F32, "xh") for _ in range(2)]
    tps = [cx.ps([128, 4, 128], F32, "tp") for _ in range(2)]
    scp = cx.sb([128, 2, 16], F32, "scp")
    S.op("dve", lambda e: e.tensor_scalar_add(out=scp[:, 0, :], in0=modfm[:, 0, :], scalar1=1.0), reads=[mkey],
         writes=[("scp", 0)])
    S.op("dve", lambda e: e.tensor_scalar_add(out=scp[:, 1, :], in0=modfm[:, 2, :], scalar1=1.0), reads=[mkey],
         writes=[("scp", 1)])
    n = 0
    import os
    for t, (t0, rows) in enumerate(CH[:int(os.environ.get("NCH", "9"))]):
        xt, xh = xts[t % 2], xhs[t % 2]
        S.dma("sp", xt[:rows, :], x_dram[t0:t0 + rows, :], writes=[("xt", t % 2)])
        layer_norm_chunk(S, cx, xt, ("xt", t % 2), xh, ("xh", t % 2), rows, lt, tag)
        v = 0 if t < 8 else 1
        for kg in range(4):
            bk = n % 2
            n += 1
            tp = tps[bk]
            for kk in range(4):
                k = kg * 4 + kk
                S.op("pe", lambda e, xh=xh, k=k, kk=kk, rows=rows, tp=tp: e.transpose(
                    out=tp[:, kk, :rows], in_=xh[:rows, k * 128:(k + 1) * 128], identity=ident[:rows, :rows]),
                     reads=[("xh", t % 2), "ident"], writes=[("tp", bk)])
            for kk in range(4):
                k = kg * 4 + kk
                S.op("dve", lambda e, k=k, kk=kk, rows=rows, t0=t0, v=v, tp=tp: e.tensor_scalar(
                    out=uT[:, k, t0:t0 + rows], in0=tp[:, kk, :rows], scalar1=scp[:, v, k:k + 1],
                    scalar2=modfm[:, 1 + 2 * v, k:k + 1], op0=ALU.mult, op1=ALU.add),
                     reads=[("tp", bk), ("scp", v), mkey], writes=[("uT", t, k)])


def build_A(stop=0):
    nc = bass.Bass("TRN2", target_bir_lowering=False)
    dram = lambda n, s, d, k="ExternalInput": nc.dram_tensor(n, list(s), d, kind=k).ap()
    xin = dram("xin", [TT, D], F32)
    modfm_d = dram("modfm", [128, 4, 16], F32)
    w_in = dram("w_in", [D, INW], F32)
    w_pm = dram("w_pm", [D, 2048], F32)
    cos_d = dram("cos", [128, TL], F32)
    sin_d = dram("sin", [128, TL], F32)
    ident_d = dram("ident", [128, 128], F32)
    qr_o = dram("qr", [NH, 128, TL], BF16, "ExternalOutput")
    qu_o = dram("qu", [NH, 128, TT], BF16, "ExternalOutput")
    kk_o = dram("kk", [NH, 128, TT], BF16, "ExternalOutput")
    v_o = dram("v", [TT, NAW], BF16, "ExternalOutput")
    xb_o = dram("xb", [LRUW, TT], F32, "ExternalOutput")
    g_o = dram("g", [LRUW, TT], F32, "ExternalOutput")
    fx_o = dram("fx", [TT, FNW], BF16, "ExternalOutput")
    with ExitStack() as es:
        cx = Ctx(nc, es)
        S = Sched(nc)
        ident = cx.sb([128, 128], F32, "ident")
        modfm = cx.sb([128, 4, 16], F32, "modfm")
        cos = cx.sb([128, TL], F32, "cos")
        sin = cx.sb([128, TL], F32, "sin")
        uT = cx.sb([128, 16, TT], BF16, "uT")
        S.dma("sp", ident[:], ident_d[:, :], writes=["ident"])
        S.dma("sp", modfm[:], modfm_d[:, :, :], writes=["modfm"])
        S.dma("sp", cos[:], cos_d[:, :], writes=["cos"])
        S.dma("sp", sin[:], sin_d[:, :], writes=["sin"])
        lt = ln_tmp(cx, S, "a")
        slabs = [cx.sb([128, 16, 512], BF16, "slab") for _ in range(4)]
        nslab = [0]

        def load_slab(src):
            i = nslab[0] % 4
            nslab[0] += 1
            S.dma("pool", slabs[i][:], src.rearrange("(k p) n -> p k n", p=128), writes=[("slab", i)])
            return i

        pre = [load_slab(w_in[:, 0:512]), load_slab(w_pm[:, 0:512])] if stop != 2 else None
        ln_transpose_phase(S, cx, xin, modfm, uT, ident, lt, "a")
        uT_keys = [("uT", t, k) for t in range(9) for k in range(16)]
        if stop in (1, 2):
            S.dma("sp", xb_o[0:128, :], uT[:, 0:2, :].bitcast(F32).rearrange("p a b -> p (a b)"), reads=uT_keys, is_output=True)
            S.emit()
            return nc
        pss = [cx.ps([128, 512], F32, "ps") for _ in range(6)]
        npp = [0]

        def next_ps():
            i = npp[0] % 6
            npp[0] += 1
            return i

        qs = [cx.sb([128, 512], F32, "qs") for _ in range(2)]
        qp = [cx.sb([128, 512], F32, "qp") for _ in range(2)]
        t1 = [cx.sb([128, 512], F32, "t1") for _ in range(2)]
        st_r = [cx.sb([128, TL], BF16, "st_r") for _ in range(2)]
        st_u = [cx.sb([128, TT], BF16, "st_u") for _ in range(2)]
        st_f = [cx.sb([128, TT], F32, "st_f") for _ in range(2)]
        st_t = [cx.sb([128, 512], BF16, "st_t") for _ in range(3)]
        ne = [0]
        for s in range(4):
            if s == 0:
                ia, ib = pre
            else:
                ia = load_slab(w_in[:, s * 512:(s + 1) * 512])
                ib = load_slab(w_pm[:, s * 512:(s + 1) * 512])
            for cc in range(4):
                h = (s % 2) * 4 + cc
                hi = (s * 4 + cc) % 2
                for tb, (t0, n) in enumerate(TB):
                    pa, pb = next_ps(), next_ps()
                    for k in range(16):
                        S.op("pe", lambda e, pa=pa, ia=ia, k=k, cc=cc, t0=t0, n=n: e.matmul(
                            pss[pa][:, :n], slabs[ia][:, k, cc * 128:(cc + 1) * 128], uT[:, k, t0:t0 + n],
                            start=(k == 0), stop=(k == 15)),
                             reads=[("slab", ia)] + uT_keys, writes=[("ps", pa)])
                    if tb < 2:
                        for k in range(16):
                            S.op("pe", lambda e, pb=pb, ib=ib, k=k, cc=cc, t0=t0, n=n: e.matmul(
                                pss[pb][:, :n], slabs[ib][:, k, cc * 128:(cc + 1) * 128], uT[:, k, t0:t0 + n],
                                start=(k == 0), stop=(k == 15)),
                                 reads=[("slab", ib)] + uT_keys, writes=[("ps", pb)])
                        j = ne[0] % 2
                        ne[0] += 1
                        S.op("act", lambda e, j=j, pa=pa: e.activation(out=qs[j][:], in_=pss[pa][:], func=AF.Copy),
                             reads=[("ps", pa)], writes=[("qs", j)])
                        S.op("act", lambda e, j=j, pb=pb: e.activation(out=qp[j][:], in_=pss[pb][:], func=AF.Copy),
                             reads=[("ps", pb)], writes=[("qp", j)])
                        if s < 2:
                            S.op("dve", lambda e, j=j, hi=hi, t0=t0, n=n: e.tensor_copy(out=st_u[hi][:, t0:t0 + n],
                                                                                      in_=qs[j][:]),
                                 reads=[("qs", j)], writes=[("st_u", hi, tb)])
                        S.op("dve", lambda e, j=j, t0=t0, n=n: e.tensor_tensor(out=t1[j][:], in0=qs[j][:],
                                                                              in1=cos[:, t0:t0 + n], op=ALU.mult),
                             reads=[("qs", j), "cos"], writes=[("t1", j)])
                        S.op("dve", lambda e, j=j, t0=t0, n=n: e.tensor_tensor(out=qp[j][:], in0=qp[j][:],
                                                                              in1=sin[:, t0:t0 + n], op=ALU.mult),
                             reads=[("qp", j), "sin"], writes=[("qp", j)])
                        dst = st_r[hi] if s < 2 else st_u[hi]
                        dkey = ("st_r", hi, tb) if s < 2 else ("st_u", hi, tb)
                        S.op("dve", lambda e, j=j, dst=dst, t0=t0, n=n: e.tensor_tensor(out=dst[:, t0:t0 + n],
                                                                                       in0=t1[j][:], in1=qp[j][:],
                                                                                       op=ALU.add),
                             reads=[("t1", j), ("qp", j)], writes=[dkey])
                    else:
                        S.op("act", lambda e, pa=pa, hi=hi, t0=t0, n=n: e.activation(out=st_u[hi][:, t0:t0 + n],
                                                                                    in_=pss[pa][:, :n], func=AF.Copy),
                             reads=[("ps", pa)], writes=[("st_u", hi, tb)])
                if s < 2:
                    S.dma("sp", qr_o[h], st_r[hi][:], reads=[("st_r", hi, 0), ("st_r", hi, 1)], is_output=True)
                    S.dma("sp", qu_o[h], st_u[hi][:], reads=[("st_u", hi, tb) for tb in range(3)], is_output=True)
                else:
                    S.dma("sp", kk_o[h], st_u[hi][:], reads=[("st_u", hi, tb) for tb in range(3)], is_output=True)
        nt = [0]
        for (c0, dst, dc0) in ((2048, v_o, 0), (2560, v_o, 512), (4096, fx_o, 0)):
            ia = load_slab(w_in[:, c0:c0 + 512])
            for t, (t0, rows) in enumerate(CH):
                pa = next_ps()
                for k in range(16):
                    S.op("pe", lambda e, pa=pa, ia=ia, k=k, t0=t0, rows=rows: e.matmul(
                        pss[pa][:rows, :], uT[:, k, t0:t0 + rows], slabs[ia][:, k, :], start=(k == 0), stop=(k == 15)),
                         reads=[("slab", ia)] + [("uT", t, k_) for k_ in range(16)], writes=[("ps", pa)])
                j = nt[0] % 3
                nt[0] += 1
                if nt[0] % 2:
                    S.op("act", lambda e, j=j, pa=pa, rows=rows: e.activation(out=st_t[j][:rows, :],
                                                                             in_=pss[pa][:rows, :], func=AF.Copy),
                         reads=[("ps", pa)], writes=[("st_t", j)])
                else:
                    S.op("dve", lambda e, j=j, pa=pa, rows=rows: e.tensor_copy(out=st_t[j][:rows, :],
                                                                              in_=pss[pa][:rows, :]),
                         reads=[("ps", pa)], writes=[("st_t", j)])
                S.dma("sp", dst[t0:t0 + rows, dc0:dc0 + 512], st_t[j][:rows, :], reads=[("st_t", j)], is_output=True)
        nf = [0]
        for (c0, dst) in ((3072, xb_o), (3584, g_o)):
            ia = load_slab(w_in[:, c0:c0 + 512])
            for cc in range(4):
                j = nf[0] % 2
                nf[0] += 1
                for tb, (t0, n) in enumerate(TB):
                    pa = next_ps()
                    for k in range(16):
                        S.op("pe", lambda e, pa=pa, ia=ia, k=k, cc=cc, t0=t0, n=n: e.matmul(
                            pss[pa][:, :n], slabs[ia][:, k, cc * 128:(cc + 1) * 128], uT[:, k, t0:t0 + n],
                            start=(k == 0), stop=(k == 15)),
                             reads=[("slab", ia)] + uT_keys, writes=[("ps", pa)])
                    if tb % 2:
                        S.op("act", lambda e, j=j, pa=pa, t0=t0, n=n: e.activation(out=st_f[j][:, t0:t0 + n],
                                                                                  in_=pss[pa][:, :n], func=AF.Copy),
                             reads=[("ps", pa)], writes=[("st_f", j, tb)])
                    else:
                        S.op("dve", lambda e, j=j, pa=pa, t0=t0, n=n: e.tensor_copy(out=st_f[j][:, t0:t0 + n],
                                                                                   in_=pss[pa][:, :n]),
                             reads=[("ps", pa)], writes=[("st_f", j, tb)])
                S.dma("sp", dst[cc * 128:(cc + 1) * 128, :], st_f[j][:], reads=[("st_f", j, tb) for tb in range(3)],
                      is_output=True)
        S.emit()
    return nc


def rope_tables(core):
    j = core % 4
    t = np.arange(TL) + j * TL
    rows, cols = t // GW, t % GW
    quarter = 32
    inv = 10000.0 ** (-np.arange(quarter, dtype=np.float64) / quarter)
    cos = np.zeros((128, TL)); sin = np.zeros((128, TL))
    for d in range(128):
        pos = rows if d < 64 else cols
        i = d % 32
        ang = pos * inv[i]
        cos[d] = np.cos(ang)
        sin[d] = -np.sin(ang) if (d % 64) < 32 else np.sin(ang)
    return cos.astype(np.float32), sin.astype(np.float32)


def perm_cols():
    idx = np.arange(2048)
    d = idx % 128
    partner = np.where((d % 64) < 32, d + 32, d - 32)
    return (idx // 128) * 128 + partner


def fm16(v):
    return np.ascontiguousarray(v.reshape(16, 128).T)


def core_rows(x, ctx, core):
    b, j = core // 4, core % 4
    return np.concatenate([x[b, j * TL:(j + 1) * TL], ctx[b, j * TC:(j + 1) * TC]], axis=0)


_IDENT = np.eye(128, dtype=np.float32).astype(ml_dtypes.bfloat16)
_IDENTF = np.eye(128, dtype=np.float32)


def run_A(x, ctx, mod_l, w_in_l, cores=None):
    w_pm = np.ascontiguousarray(w_in_l[:, perm_cols()])
    in_maps = []
    cores = list(range(NCORES)) if cores is None else cores
    for core in cores:
        b = core // 4
        sh1, sc1 = mod_l[b, 0:D], mod_l[b, D:2 * D]
        csh1, csc1 = mod_l[2, 0:D], mod_l[2, D:2 * D]
        modfm = np.ascontiguousarray(np.stack([fm16(sc1), fm16(sh1), fm16(csc1), fm16(csh1)], axis=1))
        cos, sin = rope_tables(core)
        in_maps.append({"xin": core_rows(x, ctx, core), "modfm": modfm, "w_in": w_in_l, "w_pm": w_pm,
                        "cos": cos, "sin": sin, "ident": _IDENTF})
    nc = build_A()
    res = run_bass_kernel_spmd(nc, in_maps, core_ids=list(range(len(cores))))
    return res.results


def build_B2():
    nc = bass.Bass("TRN2", target_bir_lowering=False)
    dram = lambda n, s, d, k="ExternalInput": nc.dram_tensor(n, list(s), d, kind=k).ap()
    x1 = dram("x1", [TT, D], F32)
    modfm_d = dram("modfm", [128, 4, 16], F32)
    w1 = dram("w1", [D, DFF], F32)
    b1_d = dram("b1", [128, 64], F32)
    w2 = dram("w2", [DFF, D], F32)
    bc_d = dram("bc", [5, D], F32)
    ident_d = dram("ident", [128, 128], F32)
    x2 = dram("x2", [TT, D], F32, "ExternalOutput")
    with ExitStack() as es:
        cx = Ctx(nc, es)
        S = Sched(nc)
        ident = cx.sb([128, 128], F32, "ident")
        modfm = cx.sb([128, 4, 16], F32, "modfm")
        b1 = cx.sb([128, 64], F32, "b1")
        vT = cx.sb([128, 16, TT], BF16, "vT")
        acc = cx.sb([128, 9, D], F32, "acc")
        S.dma("sp", ident[:], ident_d[:, :], writes=["ident"])
        S.dma("sp", modfm[:], modfm_d[:, :, :], writes=["modfm"])
        S.dma("sp", b1[:], b1_d[:, :], writes=["b1"])
        lt = ln_tmp(cx, S, "m")
        pss = [cx.ps([128, 512], F32, "ps") for _ in range(6)]
        npp = [0]

        def next_ps():
            i = npp[0] % 6
            npp[0] += 1
            return i

        with ExitStack() as es1:
            ln_transpose_phase(S, Ctx(nc, es1), x1, modfm, vT, ident, lt, "m")
            S.fence()
        vT_keys = [("uT", t, k) for t in range(9) for k in range(16)]
        with ExitStack() as es2:
            c2 = Ctx(nc, es2)
            slabs = [c2.sb([128, 16, 512], BF16, "slab") for _ in range(3)]
            hTs = [c2.sb([128, 8, TT], BF16, "hT") for _ in range(2)]
            tmps = [c2.sb([128, 512], F32, "tmp") for _ in range(2)]
            nslab = [0]
            ntmp = [0]
            for fb in range(8):
                hT = hTs[fb % 2]
                for half in range(2):
                    i = nslab[0] % 3
                    nslab[0] += 1
                    c0 = fb * 1024 + half * 512
                    S.dma("pool", slabs[i][:], w1[:, c0:c0 + 512].rearrange("(k p) n -> p k n", p=128),
                          writes=[("slab", i)])
                    for cc in range(4):
                        fc = fb * 8 + half * 4 + cc
                        hk = half * 4 + cc
                        for tb, (t0, n) in enumerate(TB):
                            pa = next_ps()
                            for k in range(16):
                                S.op("pe", lambda e, pa=pa, i=i, k=k, cc=cc, t0=t0, n=n: e.matmul(
                                    pss[pa][:, :n], slabs[i][:, k, cc * 128:(cc + 1) * 128], vT[:, k, t0:t0 + n],
                                    start=(k == 0), stop=(k == 15)),
                                     reads=[("slab", i)] + vT_keys, writes=[("ps", pa)])
                            j = ntmp[0] % 2
                            ntmp[0] += 1
                            S.op("act", lambda e, j=j, pa=pa, n=n, fc=fc: e.activation(
                                out=tmps[j][:, :n], in_=pss[pa][:, :n], func=AF.Relu, bias=b1[:, fc:fc + 1], scale=1.0),
                                 reads=[("ps", pa), "b1"], writes=[("tmp", j)])
                            S.op("dve", lambda e, j=j, hT=hT, hk=hk, t0=t0, n=n: e.tensor_tensor(
                                out=hT[:, hk, t0:t0 + n], in0=tmps[j][:, :n], in1=tmps[j][:, :n], op=ALU.mult),
                                 reads=[("tmp", j)], writes=[("hT", fb % 2, hk, tb)])
                hkeys = [("hT", fb % 2, hk, tb) for hk in range(8) for tb in range(3)]
                for nn in range(4):
                    i = nslab[0] % 3
                    nslab[0] += 1
                    S.dma("pool", slabs[i][:, 0:8, :],
                          w2[fb * 1024:(fb + 1) * 1024, nn * 512:(nn + 1) * 512].rearrange("(k p) n -> p k n", p=128),
                          writes=[("slab", i)])
                    for t, (t0, rows) in enumerate(CH):
                        pa = next_ps()
                        for k in range(8):
                            S.op("pe", lambda e, pa=pa, i=i, k=k, hT=hT, t0=t0, rows=rows: e.matmul(
                                pss[pa][:rows, :], hT[:, k, t0:t0 + rows], slabs[i][:, k, :], start=(k == 0),
                                stop=(k == 7)),
                                 reads=[("slab", i)] + hkeys, writes=[("ps", pa)])
                        if fb == 0:
                            S.op("act", lambda e, pa=pa, t=t, nn=nn, rows=rows: e.activation(
                                out=acc[:rows, t, nn * 512:(nn + 1) * 512], in_=pss[pa][:rows, :], func=AF.Copy),
                                 reads=[("ps", pa)], writes=[("acc", t, nn)])
                        else:
                            S.op("dve", lambda e, pa=pa, t=t, nn=nn, rows=rows: e.tensor_tensor(
                                out=acc[:rows, t, nn * 512:(nn + 1) * 512], in0=pss[pa][:rows, :],
                                in1=acc[:rows, t, nn * 512:(nn + 1) * 512], op=ALU.add),
                                 reads=[("ps", pa), ("acc", t, nn)], writes=[("acc", t, nn)])
            S.fence()
        with ExitStack() as es3:
            c3 = Ctx(nc, es3)
            bc = c3.sb([128, 5, D], F32, "bc")
            for r in range(5):
                S.dma("sp", bc[:, r, :], bc_d[r].partition_broadcast(128), writes=[("bc", r)])
            xts = [c3.sb([128, D], F32, "xt") for _ in range(2)]
            zts = [c3.sb([128, D], F32, "zt") for _ in range(2)]
            residual_ln_phase(S, x1, x2, acc, bc, xts, zts, lt, "m2", has_bias=True)
        S.emit()
    return nc


def residual_ln_phase(S, x_dram, out_dram, acc, bc, xts, zts, lt, tag, has_bias, ts=None, accmod=9):
    for tt in (range(9) if ts is None else ts):
        t0, rows = CH[tt]
        t = tt % accmod
        xt, zt = xts[tt % 2], zts[tt % 2]
        v = 1 if tt < 8 else 2
        S.dma("sp", xt[:rows, :], x_dram[t0:t0 + rows, :], writes=[("rxt", t % 2)])
        akeys = [("acc", t, nn) for nn in range(4)]
        if has_bias:
            S.op("dve", lambda e, t=t, rows=rows: e.tensor_tensor(out=acc[:rows, t, :], in0=acc[:rows, t, :],
                                                                 in1=bc[:rows, 0, :], op=ALU.add),
                 reads=akeys + [("bc", 0)], writes=akeys)
        S.op("dve", lambda e, t=t, rows=rows, v=v: e.tensor_tensor(out=acc[:rows, t, :], in0=acc[:rows, t, :],
                                                                  in1=bc[:rows, v, :], op=ALU.mult),
             reads=akeys + [("bc", v)], writes=akeys)
        S.op("dve", lambda e, t=t, rows=rows, xt=xt, zt=zt: e.scalar_tensor_tensor(
            out=zt[:rows, :], in0=xt[:rows, :], scalar=ALPHA, in1=acc[:rows, t, :], op0=ALU.mult, op1=ALU.add),
             reads=akeys + [("rxt", t % 2)], writes=[("rzt", t % 2)])
        layer_norm_chunk(S, None, zt, ("rzt", t % 2), xt, ("rxt", t % 2), rows, lt, tag)
        S.op("dve", lambda e, rows=rows, xt=xt: e.tensor_tensor(out=xt[:rows, :], in0=xt[:rows, :], in1=bc[:rows, 3, :],
                                                              op=ALU.mult),
             reads=[("rxt", t % 2), ("bc", 3)], writes=[("rxt", t % 2)])
        S.op("dve", lambda e, rows=rows, xt=xt, zt=zt: e.tensor_tensor(out=zt[:rows, :], in0=xt[:rows, :],
                                                                      in1=bc[:rows, 4, :], op=ALU.add),
             reads=[("rxt", t % 2), ("bc", 4)], writes=[("rzt", t % 2)])
        S.dma("sp", out_dram[t0:t0 + rows, :], zt[:rows, :], reads=[("rzt", t % 2)], is_output=True)


def run_B2(x1rows, mod_l, w1, b1, w2, b2, lng, lnb, cores=None):
    cores = list(range(NCORES)) if cores is None else cores
    in_maps = []
    for ci, core in enumerate(cores):
        b = core // 4
        sh2, sc2, g2 = mod_l[b, 3 * D:4 * D], mod_l[b, 4 * D:5 * D], mod_l[b, 5 * D:6 * D]
        csh2, csc2, cg2 = mod_l[2, 3 * D:4 * D], mod_l[2, 4 * D:5 * D], mod_l[2, 5 * D:6 * D]
        modfm = np.ascontiguousarray(np.stack([fm16(sc2), fm16(sh2), fm16(csc2), fm16(csh2)], axis=1))
        bc = np.ascontiguousarray(np.stack([b2, g2, cg2, lng, lnb]).astype(np.float32))
        in_maps.append({"x1": x1rows[ci], "modfm": modfm, "w1": w1, "b1": np.ascontiguousarray(b1.reshape(64, 128).T),
                        "w2": w2, "bc": bc, "ident": _IDENTF})
    nc = build_B2()
    res = run_bass_kernel_spmd(nc, in_maps, core_ids=list(range(len(cores))))
    return [np.asarray(r["x2"]) for r in res.results]


PAIRS = [(0, 0, 6), (1, 1, 5), (2, 2, 5), (2, 3, 5), (2, 4, 5), (2, 5, 5), (3, 6, 5), (4, 6, 6)]
LSEQ = CTX + SEQ
XOFF_C = 2
XOFF_L = 2 + CTX + 1 + 2
XLEN = XOFF_L + SEQ + 1


def build_B1():
    nc = bass.Bass("TRN2", target_bir_lowering=False)
    dram = lambda n, s, d, k="ExternalInput": nc.dram_tensor(n, list(s), d, kind=k).ap()
    xres = dram("xres", [TT, D], F32)
    qr_d = dram("qr", [NH, 128, TL], BF16)
    qu_d = dram("qu", [NH, 128, TT], BF16)
    kw_d = dram("kwin", [NH, 128, 1792], BF16)
    vw_d = dram("vwin", [NH, 128, 14, 130], BF16)
    tab_d = dram("tab", [NH, 128, 5, 6, 128], F32)
    xb_d = dram("xbf", [4, 128, LSEQ], F32)
    g_d = dram("gT", [4, 128, TT], F32)
    seg_d = dram("seg", [128, 4], F32)
    cw_d = dram("convw", [128, 4, 5], F32)
    wa_d = dram("wa", [2, 4, 128, 128], F32)
    wx_d = dram("wx", [2, 4, 128, 128], F32)
    gb_d = dram("gb", [128, 3, 2, 4], F32)
    fx_d = dram("fxf", [34, 128, FNW], BF16)
    cm_d = dram("cosm", [32, 128, TL], BF16)
    sm_d = dram("sinm", [32, 128, TL], BF16)
    cc_d = dram("cosc", [2, 128, TC], BF16)
    sc_d = dram("sinc", [2, 128, TC], BF16)
    dc_d = dram("dftc", [2, 128, 128], BF16)
    fw_d = dram("fnow", [FNW, FNW], F32)
    fb_d = dram("fnob", [128, 4], F32)
    wo_d = dram("wout", [D, D], F32)
    bc_d = dram("bc", [5, D], F32)
    identb_d = dram("identb", [128, 128], BF16)
    x1_o = dram("x1", [TT, D], F32, "ExternalOutput")
    SCALE = HD ** -0.5
    with ExitStack() as es:
        cx = Ctx(nc, es)
        S = Sched(nc)
        identb = cx.sb([128, 128], BF16, "identb")
        catT = cx.sb([128, 16, TT], BF16, "catT")
        S.dma("sp", identb[:], identb_d[:, :], writes=["identb"])
        lt = ln_tmp(cx, S, "b")
        banks = [cx.ps([128, 512], F32, "bank") for _ in range(8)]
        with ExitStack() as es1:
            c1 = Ctx(nc, es1)
            kh = [c1.sb([128, 1792], BF16, "kh") for _ in range(2)]
            vh = [c1.sb([128, 14, 130], BF16, "vh") for _ in range(2)]
            qrh = [c1.sb([128, TL], BF16, "qrh") for _ in range(2)]
            quh = [c1.sb([128, TT], BF16, "quh") for _ in range(2)]
            tabh = [c1.sb([128, 5, 6, 128], F32, "tabh") for _ in range(2)]
            pf = [c1.sb([128, 6, 128], F32, "pf") for _ in range(2)]
            pb = [c1.sb([128, 8, 128], BF16, "pb") for _ in range(2)]
            ob = [c1.sb([128, 128], BF16, "ob") for _ in range(2)]
            rc = [c1.sb([128, 1], F32, "rc") for _ in range(2)]
            it = 0
            for h in range(NH):
                hb = h % 2
                S.dma("sp", kh[hb][:], kw_d[h], writes=[("kh", hb)])
                S.dma("sp", vh[hb][:], vw_d[h], writes=[("vh", hb)])
                S.dma("sp", qrh[hb][:], qr_d[h], writes=[("qrh", hb)])
                S.dma("sp", quh[hb][:], qu_d[h], writes=[("quh", hb)])
                S.dma("sp", tabh[hb][:], tab_d[h], writes=[("tabh", hb)])
                for i in range(9):
                    j = it % 2
                    it += 1
                    sA, sB, po, pt = banks[4 * j], banks[4 * j + 1], banks[4 * j + 2], banks[4 * j + 3]
                    sAv = sA[:, :].rearrange("p (a b) -> p a b", a=4)
                    sBv = sB[:, :].rearrange("p (a b) -> p a b", a=4)
                    if i < 8:
                        cls, s0, nch = PAIRS[i]
                        nq = 128
                        q0 = i * 128
                        for jj in range(nch):
                            dst = sAv[:, jj, :] if jj < 4 else sBv[:, jj - 4, :]
                            S.op("pe", lambda e, dst=dst, hb=hb, c=s0 + jj, q0=q0: e.matmul(
                                dst, kh[hb][:, c * 128:(c + 1) * 128], qrh[hb][:, q0:q0 + 128], start=True, stop=True),
                                 reads=[("kh", hb), ("qrh", hb)], writes=[("bk", 4 * j + (0 if jj < 4 else 1))])
                    else:
                        nch = 0
                        nq = 64
                        q0 = 1024
                    for c in range(2):
                        S.op("pe", lambda e, hb=hb, c=c, q0=q0, nq=nq, sBv=sBv: e.matmul(
                            sBv[:, 2 + c, :nq], kh[hb][:, 1536 + c * 128:1536 + (c + 1) * 128], quh[hb][:, q0:q0 + nq],
                            start=True, stop=True),
                             reads=[("kh", hb), ("quh", hb)], writes=[("bk", 4 * j + 1)])
                    if i < 8:
                        S.op("dve", lambda e, j=j, hb=hb, cls=cls, sAv=sAv: e.scalar_tensor_tensor(
                            out=pf[j][:, 0:4, :], in0=sAv[:, 0:4, :], scalar=SCALE, in1=tabh[hb][:, cls, 0:4, :],
                            op0=ALU.mult, op1=ALU.add),
                             reads=[("bk", 4 * j), ("tabh", hb)], writes=[("pf", j, 0)])
                        S.op("dve", lambda e, j=j, hb=hb, cls=cls, sBv=sBv, nch=nch: e.scalar_tensor_tensor(
                            out=pf[j][:, 4:nch, :], in0=sBv[:, 0:nch - 4, :], scalar=SCALE,
                            in1=tabh[hb][:, cls, 4:nch, :], op0=ALU.mult, op1=ALU.add),
                             reads=[("bk", 4 * j + 1), ("tabh", hb)], writes=[("pf", j, 1)])
                        S.op("act", lambda e, j=j, nch=nch: e.activation(out=pb[j][:, 0:nch, :], in_=pf[j][:, 0:nch, :],
                                                                        func=AF.Exp),
                             reads=[("pf", j, 0), ("pf", j, 1)], writes=[("pb", j, 0)])
                    S.op("act", lambda e, j=j, sBv=sBv, nq=nq: e.activation(out=pb[j][:, 6:8, :nq], in_=sBv[:, 2:4, :nq],
                                                                          func=AF.Exp, scale=SCALE),
                         reads=[("bk", 4 * j + 1)], writes=[("pb", j, 1)])
                    nmm = nch + 2
                    for m in range(nmm):
                        if m < nch:
                            slot, vc = m, s0 + m
                        else:
                            slot, vc = 6 + (m - nch), 12 + (m - nch)
                        S.op("pe", lambda e, j=j, hb=hb, slot=slot, vc=vc, nq=nq, m=m, nmm=nmm, po=po: e.matmul(
                            po[:nq, 0:130], pb[j][:, slot, :nq], vh[hb][:, vc, :], start=(m == 0), stop=(m == nmm - 1)),
                             reads=[("pb", j, 0), ("pb", j, 1), ("vh", hb)], writes=[("bk", 4 * j + 2)])
                    S.op("dve", lambda e, j=j, nq=nq, po=po: e.reciprocal(out=rc[j][:nq, :], in_=po[:nq, 128:129]),
                         reads=[("bk", 4 * j + 2)], writes=[("rc", j)])
                    S.op("dve", lambda e, j=j, nq=nq, po=po: e.tensor_scalar(out=ob[j][:nq, :], in0=po[:nq, 0:128],
                                                                             scalar1=rc[j][:nq, 0:1], scalar2=None,
                                                                             op0=ALU.mult),
                         reads=[("bk", 4 * j + 2), ("rc", j)], writes=[("ob", j)])
                    ptv = pt[:, 0:64].bitcast(BF16)
                    S.op("pe", lambda e, j=j, nq=nq, ptv=ptv: e.transpose(out=ptv[:, :nq], in_=ob[j][:nq, :],
                                                                          identity=identb[:nq, :nq]),
                         reads=[("ob", j), "identb"], writes=[("bk", 4 * j + 3)])
                    S.op("act", lambda e, h=h, q0=q0, nq=nq, ptv=ptv: e.activation(out=catT[:, h, q0:q0 + nq],
                                                                                  in_=ptv[:, :nq], func=AF.Copy),
                         reads=[("bk", 4 * j + 3)], writes=[("catT", h, i)])
            S.fence()
        with ExitStack() as es2:
            c2 = Ctx(nc, es2)
            xbuf = c2.sb([128, XLEN], F32, "xbuf")
            xc = c2.sb([128, LSEQ], F32, "xc")
            xcb = c2.sb([128, LSEQ], BF16, "xcb")
            a_all = c2.sb([128, LSEQ], F32, "a_all")
            u_all = c2.sb([128, LSEQ], F32, "u_all")
            hs = [c2.sb([128, LSEQ], F32, "h") for _ in range(2)]
            rt = [c2.sb([128, 512], F32, "rt") for _ in range(2)]
            itl = [c2.sb([128, 512], F32, "itl") for _ in range(2)]
            a2 = [c2.sb([128, 512], F32, "a2") for _ in range(2)]
            gl = c2.sb([128, TT], F32, "gl")
            yo = c2.sb([128, TT], F32, "yo")
            seg = c2.sb([128, 4], F32, "seg")
            cw = c2.sb([128, 4, 5], F32, "cw")
            gb = c2.sb([128, 3, 2, 4], F32, "gb")
            cneg = c2.sb([128, 2, 4], F32, "cneg")
            one = c2.sb([128, 1], F32, "one")
            wab = c2.sb([128, 2, 4, 128], BF16, "wab")
            wxb = c2.sb([128, 2, 4, 128], BF16, "wxb")
            S.dma("sp", seg[:], seg_d[:, :], writes=["seg"])
            S.dma("sp", cw[:], cw_d[:, :, :], writes=["cw"])
            S.dma("sp", gb[:], gb_d[:, :, :, :], writes=["gb"])
            S.dma("pool", wab[:], wa_d.rearrange("d c i o -> i d c o"), writes=["wab"])
            S.dma("pool", wxb[:], wx_d.rearrange("d c i o -> i d c o"), writes=["wxb"])
            S.op("dve", lambda e: e.memset(one[:], 1.0), writes=["one"])
            S.op("dve", lambda e: e.memset(xbuf[:], 0.0), writes=["xbuf"])
            S.op("act", lambda e: e.activation(out=cneg[:], in_=gb[:, 2, :, :], func=AF.Exp, scale=-1.0), reads=["gb"],
                 writes=["cneg"])
            S.op("act", lambda e: e.activation(out=cneg[:], in_=cneg[:], func=AF.Ln, bias=one[:, 0:1], scale=1.0),
                 reads=["cneg", "one"], writes=["cneg"])
            S.op("dve", lambda e: e.tensor_scalar_mul(out=cneg[:], in0=cneg[:], scalar1=-8.0), reads=["cneg"],
                 writes=["cneg"])
            SEGS = [(XOFF_C, 0, CTX), (XOFF_L, CTX, SEQ)]
            BLK = [(0, 256)] + [(256 + 512 * i, 512) for i in range(8)]
            ib = 0
            for cb in range(4):
                S.dma("sp", xbuf[:, XOFF_C:XOFF_C + CTX], xb_d[cb, :, 0:CTX], reads=["xbuf"], writes=["xbuf_c"])
                S.dma("sp", xbuf[:, XOFF_L:XOFF_L + SEQ], xb_d[cb, :, CTX:LSEQ], reads=["xbuf"], writes=["xbuf_l"])
                S.dma("sp", gl[:], g_d[cb], writes=["gl"])
                for (xo, so, ln) in SEGS:
                    S.op("dve", lambda e, xo=xo, so=so, ln=ln, cb=cb: e.tensor_scalar(
                        out=xc[:, so:so + ln], in0=xbuf[:, xo - 2:xo - 2 + ln], scalar1=cw[:, cb, 0:1],
                        scalar2=cw[:, cb, 4:5], op0=ALU.mult, op1=ALU.add),
                         reads=["xbuf_c", "xbuf_l", "cw"], writes=["xc"])
                    for jt in range(1, 4):
                        S.op("dve", lambda e, xo=xo, so=so, ln=ln, cb=cb, jt=jt: e.scalar_tensor_tensor(
                            out=xc[:, so:so + ln], in0=xbuf[:, xo - 2 + jt:xo - 2 + jt + ln], scalar=cw[:, cb, jt:jt + 1],
                            in1=xc[:, so:so + ln], op0=ALU.mult, op1=ALU.add),
                             reads=["xbuf_c", "xbuf_l", "cw", "xc"], writes=["xc"])
                S.op("act", lambda e: e.activation(out=xcb[:], in_=xc[:], func=AF.Copy), reads=["xc"], writes=["xcb"])
                S.op("act", lambda e: e.activation(out=gl[:], in_=gl[:], func=AF.Gelu), reads=["gl"], writes=["gl"])
                for d in range(2):
                    for (b0, n) in BLK:
                        j = ib % 2
                        ib += 1
                        pr, pi = banks[2 * j], banks[2 * j + 1]
                        S.op("pe", lambda e, pr=pr, d=d, cb=cb, b0=b0, n=n: e.matmul(
                            pr[:, :n], wab[:, d, cb, :], xcb[:, b0:b0 + n], start=True, stop=True),
                             reads=["wab", "xcb"], writes=[("bk", 2 * j)])
                        S.op("pe", lambda e, pi=pi, d=d, cb=cb, b0=b0, n=n: e.matmul(
                            pi[:, :n], wxb[:, d, cb, :], xcb[:, b0:b0 + n], start=True, stop=True),
                             reads=["wxb", "xcb"], writes=[("bk", 2 * j + 1)])
                        S.op("act", lambda e, j=j, pr=pr, d=d, cb=cb, n=n: e.activation(
                            out=rt[j][:, :n], in_=pr[:, :n], func=AF.Sigmoid, bias=gb[:, 0, d, cb:cb + 1], scale=1.0),
                             reads=[("bk", 2 * j), "gb"], writes=[("rt", j)])
                        S.op("act", lambda e, j=j, pi=pi, d=d, cb=cb, n=n: e.activation(
                            out=itl[j][:, :n], in_=pi[:, :n], func=AF.Sigmoid, bias=gb[:, 1, d, cb:cb + 1], scale=1.0),
                             reads=[("bk", 2 * j + 1), "gb"], writes=[("itl", j)])
                        S.op("act", lambda e, j=j, d=d, cb=cb, b0=b0, n=n: e.activation(
                            out=a_all[:, b0:b0 + n], in_=rt[j][:, :n], func=AF.Exp, scale=cneg[:, d, cb:cb + 1]),
                             reads=[("rt", j), "cneg"], writes=[("a_all", b0)])
                        S.op("dve", lambda e, j=j, b0=b0, n=n: e.tensor_tensor(
                            out=a2[j][:, :n], in0=a_all[:, b0:b0 + n], in1=a_all[:, b0:b0 + n], op=ALU.mult),
                             reads=[("a_all", b0)], writes=[("a2", j)])
                        S.op("act", lambda e, j=j, n=n: e.activation(out=a2[j][:, :n], in_=a2[j][:, :n], func=AF.Sqrt,
                                                                    bias=one[:, 0:1], scale=-1.0),
                             reads=[("a2", j), "one"], writes=[("a2", j)])
                        S.op("dve", lambda e, j=j, n=n: e.tensor_tensor(out=itl[j][:, :n], in0=itl[j][:, :n],
                                                                       in1=a2[j][:, :n], op=ALU.mult),
                             reads=[("a2", j), ("itl", j)], writes=[("itl", j)])
                        S.op("dve", lambda e, j=j, b0=b0, n=n: e.tensor_tensor(out=u_all[:, b0:b0 + n],
                                                                              in0=itl[j][:, :n], in1=xc[:, b0:b0 + n],
                                                                              op=ALU.mult),
                             reads=[("itl", j), "xc"], writes=[("u_all", b0)])
                    akeys = [("a_all", b0) for (b0, n) in BLK] + [("u_all", b0) for (b0, n) in BLK]
                    hd = hs[d]
                    if d == 0:
                        S.op("dve", lambda e, hd=hd: e.tensor_tensor_scan(
                            out=hd[:, 0:CTX], data0=a_all[:, 0:CTX], data1=u_all[:, 0:CTX], initial=0.0, op0=ALU.mult,
                            op1=ALU.add), reads=akeys, writes=[("h", d, 0)])
                        S.op("dve", lambda e, hd=hd: e.tensor_tensor_scan(
                            out=hd[:, CTX:LSEQ], data0=a_all[:, CTX:LSEQ], data1=u_all[:, CTX:LSEQ],
                            initial=hd[:, CTX - 1:CTX], op0=ALU.mult, op1=ALU.add),
                             reads=akeys + [("h", d, 0)], writes=[("h", d, 1)])
                    else:
                        S.op("dve", lambda e, hd=hd: e.tensor_tensor_scan(
                            out=hd[:, 0:CTX][:, ::-1], data0=a_all[:, 0:CTX][:, ::-1], data1=u_all[:, 0:CTX][:, ::-1],
                            initial=0.0, op0=ALU.mult, op1=ALU.add), reads=akeys, writes=[("h", d, 0)])
                        S.op("dve", lambda e, hd=hd: e.tensor_tensor_scan(
                            out=hd[:, CTX:LSEQ][:, ::-1], data0=a_all[:, CTX:LSEQ][:, ::-1],
                            data1=u_all[:, CTX:LSEQ][:, ::-1], initial=hd[:, 0:1], op0=ALU.mult, op1=ALU.add),
                             reads=akeys + [("h", d, 0)], writes=[("h", d, 1)])
                hk = [("h", d, p) for d in range(2) for p in range(2)]
                S.op("dve", lambda e: e.tensor_tensor(out=hs[0][:], in0=hs[0][:], in1=hs[1][:], op=ALU.add), reads=hk,
                     writes=hk)
                for sgi in range(4):
                    for (dst0, src0, ln) in ((0, CTX + sgi * TL, TL), (TL, sgi * TC, TC)):
                        if sgi == 0:
                            S.op("dve", lambda e, dst0=dst0, src0=src0, ln=ln, sgi=sgi: e.tensor_scalar(
                                out=yo[:, dst0:dst0 + ln], in0=hs[0][:, src0:src0 + ln], scalar1=seg[:, sgi:sgi + 1],
                                scalar2=None, op0=ALU.mult), reads=hk + ["seg"], writes=[("yo", dst0)])
                        else:
                            S.op("dve", lambda e, dst0=dst0, src0=src0, ln=ln, sgi=sgi: e.scalar_tensor_tensor(
                                out=yo[:, dst0:dst0 + ln], in0=hs[0][:, src0:src0 + ln], scalar=seg[:, sgi:sgi + 1],
                                in1=yo[:, dst0:dst0 + ln], op0=ALU.mult, op1=ALU.add),
                                 reads=hk + ["seg", ("yo", dst0)], writes=[("yo", dst0)])
                S.op("dve", lambda e, cb=cb: e.tensor_tensor(out=catT[:, 8 + cb, :], in0=yo[:], in1=gl[:], op=ALU.mult),
                     reads=[("yo", 0), ("yo", TL), "gl"], writes=[("catT", 8 + cb)])
            S.fence()
        with ExitStack() as es3:
            c3 = Ctx(nc, es3)
            xall = c3.sb([128, 34, FNW], BF16, "xall")
            dft = [c3.sb([128, 2, 512], BF16, "dft") for _ in range(4)]
            dfc = c3.sb([128, 2, 2, TC], BF16, "dfc")
            dcc = c3.sb([128, 2, 128], BF16, "dcc")
            pq = [c3.sb([128, 2, 512], BF16, "pq") for _ in range(2)]
            yT = c3.sb([128, 4, TT], BF16, "yT")
            fwb = c3.sb([128, 4, FNW], BF16, "fwb")
            fb = c3.sb([128, 4], F32, "fb")
            S.dma("sp", xall[:], fx_d.rearrange("c p n -> p c n"), writes=["xall"])
            S.dma("sp", dfc[:, 0], cc_d.rearrange("c p n -> p c n"), writes=["dfc0"])
            S.dma("sp", dfc[:, 1], sc_d.rearrange("c p n -> p c n"), writes=["dfc1"])
            S.dma("sp", dcc[:], dc_d.rearrange("c p n -> p c n"), writes=["dcc"])
            S.dma("sp", fb[:], fb_d[:, :], writes=["fb"])
            S.dma("pool", fwb[:], fw_d.rearrange("(k p) n -> p k n", p=128), writes=["fwb"])
            nd = 0
            npq = 0
            for khf in range(2):
                for gp in range(2):
                    P = [banks[0], banks[1]]
                    Q = [banks[2], banks[3]]
                    for jc in range(32):
                        di = nd % 4
                        nd += 1
                        S.dma("sp", dft[di][:, 0, :], cm_d[jc, :, khf * 512:(khf + 1) * 512], writes=[("dft", di, 0)])
                        S.dma("sp", dft[di][:, 1, :], sm_d[jc, :, khf * 512:(khf + 1) * 512], writes=[("dft", di, 1)])
                        for gi in range(2):
                            g = gp * 2 + gi
                            S.op("pe", lambda e, gi=gi, g=g, jc=jc, di=di: e.matmul(
                                P[gi][:, :], xall[:, jc, g * 128:(g + 1) * 128], dft[di][:, 0, :], start=(jc == 0),
                                stop=(jc == 31)), reads=["xall", ("dft", di, 0)], writes=[("bk", gi)])
                            S.op("pe", lambda e, gi=gi, g=g, jc=jc, di=di: e.matmul(
                                Q[gi][:, :], xall[:, jc, g * 128:(g + 1) * 128], dft[di][:, 1, :], start=(jc == 0),
                                stop=(jc == 31)), reads=["xall", ("dft", di, 1)], writes=[("bk", 2 + gi)])
                    for gi in range(2):
                        g = gp * 2 + gi
                        pj = npq % 2
                        npq += 1
                        S.op("act", lambda e, gi=gi, pj=pj: e.activation(out=pq[pj][:, 0, :], in_=P[gi][:, :],
                                                                        func=AF.Copy),
                             reads=[("bk", gi)], writes=[("pq", pj, 0)])
                        S.op("dve", lambda e, gi=gi, pj=pj: e.tensor_copy(out=pq[pj][:, 1, :], in_=Q[gi][:, :]),
                             reads=[("bk", 2 + gi)], writes=[("pq", pj, 1)])
                        yb = 4 + pj
                        S.op("pe", lambda e, pj=pj, yb=yb: e.matmul(banks[yb][:, :], dcc[:, 0, :], pq[pj][:, 0, :],
                                                                   start=True, stop=False),
                             reads=["dcc", ("pq", pj, 0)], writes=[("bk", yb)])
                        S.op("pe", lambda e, pj=pj, yb=yb: e.matmul(banks[yb][:, :], dcc[:, 1, :], pq[pj][:, 1, :],
                                                                   start=False, stop=True),
                             reads=["dcc", ("pq", pj, 1)], writes=[("bk", yb)])
                        S.op("act", lambda e, g=g, yb=yb, khf=khf: e.activation(
                            out=yT[:, g, khf * 512:(khf + 1) * 512], in_=banks[yb][:, :], func=AF.Copy),
                             reads=[("bk", yb)], writes=[("yT", g, khf)])
            for g in range(4):
                for w in range(2):
                    for jc in range(2):
                        S.op("pe", lambda e, g=g, w=w, jc=jc: e.matmul(
                            banks[w][:, :TC], xall[:, 32 + jc, g * 128:(g + 1) * 128], dfc[:, w, jc, :],
                            start=(jc == 0), stop=(jc == 1)), reads=["xall", "dfc0", "dfc1"], writes=[("bk", w)])
                pj = npq % 2
                npq += 1
                S.op("act", lambda e, pj=pj: e.activation(out=pq[pj][:, 0, :TC], in_=banks[0][:, :TC], func=AF.Copy),
                     reads=[("bk", 0)], writes=[("pq", pj, 0)])
                S.op("dve", lambda e, pj=pj: e.tensor_copy(out=pq[pj][:, 1, :TC], in_=banks[1][:, :TC]),
                     reads=[("bk", 1)], writes=[("pq", pj, 1)])
                yb = 4 + pj
                S.op("pe", lambda e, pj=pj, yb=yb: e.matmul(banks[yb][:, :TC], dcc[:, 0, :], pq[pj][:, 0, :TC],
                                                           start=True, stop=False),
                     reads=["dcc", ("pq", pj, 0)], writes=[("bk", yb)])
                S.op("pe", lambda e, pj=pj, yb=yb: e.matmul(banks[yb][:, :TC], dcc[:, 1, :], pq[pj][:, 1, :TC],
                                                           start=False, stop=True),
                     reads=["dcc", ("pq", pj, 1)], writes=[("bk", yb)])
                S.op("act", lambda e, g=g, yb=yb: e.activation(out=yT[:, g, TL:TT], in_=banks[yb][:, :TC], func=AF.Copy),
                     reads=[("bk", yb)], writes=[("yT", g, 2)])
            ykeys = [("yT", g, p) for g in range(4) for p in range(3)]
            nb = 0
            for oc in range(4):
                for tb, (t0, n) in enumerate(TB):
                    bk = 6 + nb % 2
                    nb += 1
                    for g in range(4):
                        S.op("pe", lambda e, bk=bk, g=g, oc=oc, t0=t0, n=n: e.matmul(
                            banks[bk][:, :n], fwb[:, g, oc * 128:(oc + 1) * 128], yT[:, g, t0:t0 + n], start=(g == 0),
                            stop=(g == 3)), reads=["fwb"] + ykeys, writes=[("bk", bk)])
                    S.op("act", lambda e, bk=bk, oc=oc, t0=t0, n=n: e.activation(
                        out=catT[:, 12 + oc, t0:t0 + n], in_=banks[bk][:, :n], func=AF.Identity, bias=fb[:, oc:oc + 1],
                        scale=1.0), reads=[("bk", bk), "fb"], writes=[("catT", 12 + oc, tb)])
            S.fence()
        with ExitStack() as es4:
            c4 = Ctx(nc, es4)
            wsl = [c4.sb([128, 16, 512], BF16, "wsl") for _ in range(4)]
            bc = c4.sb([128, 5, D], F32, "bc")
            acc = c4.sb([128, 2, D], F32, "acc")
            xts = [c4.sb([128, D], F32, "xt") for _ in range(2)]
            zts = [c4.sb([128, D], F32, "zt") for _ in range(2)]
            for nn in range(4):
                S.dma("pool", wsl[nn][:], wo_d[:, nn * 512:(nn + 1) * 512].rearrange("(k p) n -> p k n", p=128),
                      writes=[("wsl", nn)])
            for r in range(1, 5):
                S.dma("sp", bc[:, r, :], bc_d[r].partition_broadcast(128), writes=[("bc", r)])
            nb = 0
            for t, (t0, rows) in enumerate(CH):
                for nn in range(4):
                    bk = nb % 8
                    nb += 1
                    for k in range(16):
                        S.op("pe", lambda e, bk=bk, k=k, nn=nn, t0=t0, rows=rows: e.matmul(
                            banks[bk][:rows, :], catT[:, k, t0:t0 + rows], wsl[nn][:, k, :], start=(k == 0),
                            stop=(k == 15)), reads=[("wsl", nn)], writes=[("bk", bk)])
                    S.op("act", lambda e, bk=bk, t=t, nn=nn, rows=rows: e.activation(
                        out=acc[:rows, t % 2, nn * 512:(nn + 1) * 512], in_=banks[bk][:rows, :], func=AF.Copy),
                         reads=[("bk", bk)], writes=[("acc", t % 2, nn)])
                residual_ln_phase(S, xres, x1_o, acc, bc, xts, zts, lt, "b1", has_bias=False, ts=[t], accmod=2)
        S.emit()
    return nc


_IDENTB = np.eye(128, dtype=np.float32).astype(ml_dtypes.bfloat16)
BF = ml_dtypes.bfloat16


def _bias_tables(rpb_l, core):
    j = core % 4
    base = 16 * j
    tab = np.full((NH, 128, 5, 6, 128), NEG, np.float32)
    p = np.arange(128)
    q = np.arange(128)
    reps = [0, 1, 2, 6, 7]
    for cls, i in enumerate(reps):
        _, s0, nch = PAIRS[i]
        qrow = base + 2 * i + q // 64
        qc = q % 64
        rs = np.clip(qrow - 4, 0, 56)
        cs = np.clip(qc - 8, 0, 48)
        for jj in range(nch):
            c = s0 + jj
            kr = base + 2 * c - 4 + p // 64
            kc = p % 64
            valid = ((kr[:, None] >= 0) & (kr[:, None] < 64) & (kr[:, None] >= rs[None, :]) & (kr[:, None] < rs[None, :] + 8)
                     & (kc[:, None] >= cs[None, :]) & (kc[:, None] < cs[None, :] + 16))
            dr = np.clip(kr[:, None] - qrow[None, :] + 7, 0, 14)
            dc = np.clip(kc[:, None] - qc[None, :] + 15, 0, 30)
            vals = rpb_l[:, dr, dc]
            tab[:, :, cls, jj, :] = np.where(valid[None], vals, np.float32(NEG))
    return tab


def _dft_consts(core):
    j = core % 4
    jj = np.arange(SEQ, dtype=np.float64)[:, None]
    kk = (np.arange(TL, dtype=np.float64) + j * TL)[None, :]
    ang = 2 * np.pi * ((jj * kk) % SEQ) / SEQ
    sc = 1.0 / math.sqrt(SEQ * 128)
    cosm = (np.cos(ang) * sc).astype(np.float32).astype(BF).reshape(32, 128, TL)
    sinm = (np.sin(ang) * sc).astype(np.float32).astype(BF).reshape(32, 128, TL)
    jj = np.arange(CTX, dtype=np.float64)[:, None]
    kk = (np.arange(TC, dtype=np.float64) + j * TC)[None, :]
    ang = 2 * np.pi * ((jj * kk) % CTX) / CTX
    sc = 1.0 / math.sqrt(CTX * 128)
    cosc = (np.cos(ang) * sc).astype(np.float32).astype(BF).reshape(2, 128, TC)
    sinc = (np.sin(ang) * sc).astype(np.float32).astype(BF).reshape(2, 128, TC)
    c = np.arange(128, dtype=np.float64)
    ang = 2 * np.pi * ((c[:, None] * c[None, :]) % 128) / 128
    dftc = np.stack([np.cos(ang), -np.sin(ang)]).astype(np.float32).astype(BF)
    return cosm, sinm, cosc, sinc, dftc


def run_B1(x, ctx, Ares, mod_l, W, l, cores=None):
    cores = list(range(NCORES)) if cores is None else cores
    in_maps = []
    batch_cache = {}
    for core in cores:
        b, j = core // 4, core % 4
        if b not in batch_cache:
            grp = [Ares[b * 4 + jj] for jj in range(4)]
            Kf = np.concatenate([np.asarray(g["kk"])[:, :, :TL] for g in grp], axis=2)
            Kc = np.concatenate([np.asarray(g["kk"])[:, :, TL:] for g in grp], axis=2)
            Vf = np.concatenate([np.asarray(g["v"])[:TL] for g in grp], axis=0)
            Vc = np.concatenate([np.asarray(g["v"])[TL:] for g in grp], axis=0)
            Xl = np.concatenate([np.asarray(g["xb"])[:, :TL] for g in grp], axis=1)
            Xc = np.concatenate([np.asarray(g["xb"])[:, TL:] for g in grp], axis=1)
            Fl = np.concatenate([np.asarray(g["fx"])[:TL] for g in grp], axis=0)
            Fc = np.concatenate([np.asarray(g["fx"])[TL:] for g in grp], axis=0)
            batch_cache[b] = (Kf, Kc, Vf, Vc, Xl, Xc, Fl, Fc)
        Kf, Kc, Vf, Vc, Xl, Xc, Fl, Fc = batch_cache[b]
        base = 16 * j
        kwin = np.zeros((NH, 128, 1792), BF)
        vwin = np.zeros((NH, 128, 14, 130), BF)
        for c in range(12):
            r0 = base - 4 + 2 * c
            if 0 <= r0 < 64:
                kwin[:, :, c * 128:(c + 1) * 128] = Kf[:, :, r0 * 64:r0 * 64 + 128]
                vwin[:, :, c, :128] = Vf[r0 * 64:r0 * 64 + 128].reshape(128, NH, 128).transpose(1, 0, 2)
                vwin[:, :, c, 128] = 1.0
        kwin[:, :, 1536:] = Kc
        for cc in range(2):
            vwin[:, :, 12 + cc, :128] = Vc[cc * 128:(cc + 1) * 128].reshape(128, NH, 128).transpose(1, 0, 2)
            vwin[:, :, 12 + cc, 128] = 1.0
        seg = np.zeros((128, 4), np.float32)
        seg[:, j] = 1.0
        convw = np.concatenate([W["conv_w"][l].reshape(4, 4, 128).transpose(2, 1, 0),
                                W["conv_b"][l].reshape(4, 128).T[:, :, None]], axis=2)
        gb = np.stack([W["lru_ba"][l].reshape(2, 4, 128).transpose(2, 0, 1),
                       W["lru_bx"][l].reshape(2, 4, 128).transpose(2, 0, 1),
                       W["lru_lambda"][l].reshape(2, 4, 128).transpose(2, 0, 1)], axis=1)
        cosm, sinm, cosc, sinc, dftc = _dft_consts(core)
        g1, cg1 = mod_l[b, 2 * D:3 * D], mod_l[2, 2 * D:3 * D]
        bc = np.stack([np.zeros(D, np.float32), g1, cg1, W["ln1_g"][l], W["ln1_b"][l]]).astype(np.float32)
        in_maps.append({
            "xres": core_rows(x, ctx, core), "qr": np.asarray(Ares[core]["qr"]), "qu": np.asarray(Ares[core]["qu"]),
            "kwin": kwin, "vwin": vwin, "tab": _bias_tables(W["rpb"][l], core),
            "xbf": np.ascontiguousarray(np.concatenate([Xc, Xl], axis=1).reshape(4, 128, LSEQ)),
            "gT": np.ascontiguousarray(np.asarray(Ares[core]["g"]).reshape(4, 128, TT)), "seg": seg,
            "convw": np.ascontiguousarray(convw.astype(np.float32)), "wa": W["lru_wa"][l], "wx": W["lru_wx"][l],
            "gb": np.ascontiguousarray(gb.astype(np.float32)),
            "fxf": np.ascontiguousarray(np.concatenate([Fl, Fc], axis=0).reshape(34, 128, FNW)),
            "cosm": cosm, "sinm": sinm, "cosc": cosc, "sinc": sinc, "dftc": dftc,
            "fnow": W["fno_w"][l], "fnob": np.ascontiguousarray(W["fno_b"][l].reshape(4, 128).T),
            "wout": W["w_out"][l], "bc": bc, "identb": _IDENTB})
    nc = build_B1()
    res = run_bass_kernel_spmd(nc, in_maps, core_ids=list(range(len(cores))))
    return [np.asarray(r["x1"]) for r in res.results]


def _reassemble(rows_list):
    x = np.zeros((B, SEQ, D), np.float32)
    ctx = np.zeros((B, CTX, D), np.float32)
    for core, r in enumerate(rows_list):
        b, j = core // 4, core % 4
        x[b, j * TL:(j + 1) * TL] = r[:TL]
        ctx[b, j * TC:(j + 1) * TC] = r[TL:]
    return x, ctx


def kernel(x, c, ctx, c_ctx, w_mod, b_mod, w_in, rpb, conv_w, conv_b, lru_wa, lru_ba, lru_wx, lru_bx, lru_lambda,
           fno_w, fno_b, w_out, ln1_g, ln1_b, w_fc1, b_fc1, w_fc2, b_fc2, ln2_g, ln2_b):
    f = lambda a: np.ascontiguousarray(np.asarray(a, dtype=np.float32))
    W = {"rpb": f(rpb), "conv_w": f(conv_w), "conv_b": f(conv_b), "lru_wa": f(lru_wa), "lru_ba": f(lru_ba),
         "lru_wx": f(lru_wx), "lru_bx": f(lru_bx), "lru_lambda": f(lru_lambda), "fno_w": f(fno_w), "fno_b": f(fno_b),
         "w_out": f(w_out), "ln1_g": f(ln1_g), "ln1_b": f(ln1_b)}
    x, ctx, c, c_ctx = f(x), f(ctx), f(c), f(c_ctx)
    w_in, w_fc1, w_fc2 = f(w_in), f(w_fc1), f(w_fc2)
    b_fc1, b_fc2, ln2_g, ln2_b = f(b_fc1), f(b_fc2), f(ln2_g), f(ln2_b)
    mod = run_M(c, c_ctx, f(w_mod), f(b_mod))
    for l in range(DEPTH):
        Ares = run_A(x, ctx, mod[l], w_in[l])
        Ares = {i: {k: np.asarray(v) for k, v in Ares[i].items()} for i in range(NCORES)}
        x1rows = run_B1(x, ctx, Ares, mod[l], W, l)
        del Ares
        x2rows = run_B2(x1rows, mod[l], w_fc1[l], b_fc1[l], w_fc2[l], b_fc2[l], ln2_g[l], ln2_b[l])
        x, ctx = _reassemble(x2rows)
    return x
```
